# Optimizing a Trainium2 kernel written in Bass

```python
import math
import jax, jax.numpy as jnp
from jax import lax
import numpy as np

D_MODEL = 1024
BATCH = 2
SEQ = 8192
DEPTH = 1

MEM_LEN = 256
RET_HEADS = 8
RET_DK = 64
RET_DV = 64
RET_CHUNK = 128
RET_WIDTH = RET_HEADS * RET_DV
RET_THETA_BASE = 10000.0
DIFF_HEADS = 4
DIFF_DQK = 64
DIFF_DV = 2 * DIFF_DQK
DIFF_WIDTH = DIFF_HEADS * DIFF_DV
Q_BLOCK = 128
MIX_WIDTH = RET_WIDTH + DIFF_WIDTH
RET_QK_COLS = RET_HEADS * RET_DK
DIFF_QK_COLS = DIFF_HEADS * 2 * DIFF_DQK
IN_SIZES = (RET_QK_COLS, RET_QK_COLS, RET_WIDTH, RET_WIDTH, DIFF_QK_COLS, DIFF_QK_COLS, DIFF_WIDTH)
IN_COLS = 2 * RET_QK_COLS + 2 * RET_WIDTH + 2 * DIFF_QK_COLS + DIFF_WIDTH
XATTN_HEADS = 4
XATTN_DH = D_MODEL // XATTN_HEADS
FFN_HIDDEN = -(-(8 * D_MODEL) // (3 * 256)) * 256
EPS = 1e-6

kernel_name = "hymba_retention_diffattn_memxattn_swiglu"


def rms_norm(x, gain=None):
    xf = x.astype(jnp.float32)
    y = xf * lax.rsqrt(jnp.mean(xf * xf, axis=-1, keepdims=True) + EPS)
    if gain is not None:
        y = y * gain.astype(jnp.float32)
    return y.astype(x.dtype)


def rotary(x, pos):
    half = x.shape[-1] // 2
    inv = 1.0 / (RET_THETA_BASE ** jnp.linspace(0.0, 1.0, half, dtype=jnp.float32))
    ang = pos.astype(jnp.float32)[:, None] * inv[None, :]
    cos, sin = jnp.cos(ang), jnp.sin(ang)
    x1 = x[..., :half].astype(jnp.float32)
    x2 = x[..., half:].astype(jnp.float32)
    out = jnp.concatenate([x1 * cos - x2 * sin, x2 * cos + x1 * sin], axis=-1)
    return out.astype(x.dtype)


def retention_chunkwise(q, k, v):
    B, H, S, dk = q.shape
    dv = v.shape[-1]
    C = RET_CHUNK
    N = S // C
    log_g = jnp.log1p(-(2.0 ** (-5.0 - jnp.arange(H, dtype=jnp.float32))))
    idx = jnp.arange(C, dtype=jnp.float32)
    rel = idx[:, None] - idx[None, :]
    dmask = jnp.where(rel >= 0, jnp.exp(log_g[:, None, None] * jnp.maximum(rel, 0.0)), 0.0)
    zeta = jnp.exp(log_g[:, None] * (C - 1.0 - idx))
    xi = jnp.exp(log_g[:, None] * (idx + 1.0))
    g_chunk = jnp.exp(log_g * C)

    qc = q.reshape(B, H, N, C, dk)
    kc = k.reshape(B, H, N, C, dk)
    vc = v.reshape(B, H, N, C, dv)

    scores = jnp.einsum('bhncd,bhnjd->bhncj', qc, kc) * dmask[None, :, None]
    o_intra = jnp.einsum('bhncj,bhnjv->bhncv', scores, vc)

    kv = jnp.einsum('bhncd,bhncv->nbhdv', kc * zeta[None, :, None, :, None], vc).astype(jnp.float32)

    def step(R, kv_n):
        return R * g_chunk[None, :, None, None] + kv_n, R

    _, r_prev = lax.scan(step, jnp.zeros((B, H, dk, dv), jnp.float32), kv)
    o_inter = jnp.einsum('bhncd,nbhdv->bhncv', qc, r_prev) * xi[None, :, None, :, None]
    return (o_intra + o_inter).reshape(B, H, S, dv)


def diff_attention_causal(q, k, v, lam):
    B, H, _, S, dq = q.shape
    dv = v.shape[-1]
    nb = S // Q_BLOCK
    qb = q.reshape(B, H, 2, nb, Q_BLOCK, dq).transpose(3, 0, 1, 2, 4, 5)
    kpos = jnp.arange(S)
    scale = dq ** -0.5

    def block(args):
        q_blk, i = args
        qpos = i * Q_BLOCK + jnp.arange(Q_BLOCK)
        s = jnp.einsum('bhmqd,bhmkd->bhmqk', q_blk, k).astype(jnp.float32) * scale
        s = jnp.where(kpos[None, :] <= qpos[:, None], s, -jnp.inf)
        p = jax.nn.softmax(s, axis=-1)
        a = p[:, :, 0] - lam * p[:, :, 1]
        return jnp.einsum('bhqk,bhkd->bhqd', a.astype(v.dtype), v)

    o = lax.map(block, (qb, jnp.arange(nb)))
    return o.transpose(1, 2, 0, 3, 4).reshape(B, H, S, dv)


def setup_inputs(seed: int = 0) -> dict:
    key = jax.random.key(seed)
    ks = jax.random.split(key, 24)
    f32 = jnp.float32

    def w(k, shape, fan_in):
        return jax.random.normal(k, shape, f32) * (fan_in ** -0.5)

    def gain(k, shape):
        return 1.0 + 0.02 * jax.random.normal(k, shape, f32)

    L = DEPTH
    return {
        "x": jax.random.normal(ks[0], (BATCH, SEQ, D_MODEL), f32),
        "mem": jax.random.normal(ks[1], (BATCH, MEM_LEN, D_MODEL), f32),
        "norm_mix": gain(ks[2], (L, D_MODEL)),
        "w_in": w(ks[3], (L, D_MODEL, IN_COLS), D_MODEL),
        "diff_q_gain": gain(ks[4], (L, DIFF_DQK)),
        "diff_k_gain": gain(ks[5], (L, DIFF_DQK)),
        "diff_lambda": 0.1 * jax.random.normal(ks[6], (L, 4, DIFF_DQK), f32),
        "diff_subln": gain(ks[7], (L, DIFF_DV)),
        "group_scale": gain(ks[8], (L, MIX_WIDTH)),
        "w_out": w(ks[9], (L, MIX_WIDTH, D_MODEL), MIX_WIDTH),
        "norm_x": gain(ks[10], (L, D_MODEL)),
        "norm_mem": gain(ks[11], (L, D_MODEL)),
        "xq": w(ks[12], (L, D_MODEL, D_MODEL), D_MODEL),
        "xkv": w(ks[13], (L, D_MODEL, 2 * D_MODEL), D_MODEL),
        "xq_gain": gain(ks[14], (L, XATTN_DH)),
        "xk_gain": gain(ks[15], (L, XATTN_DH)),
        "xo": w(ks[16], (L, D_MODEL, D_MODEL), D_MODEL),
        "norm_ffn": gain(ks[17], (L, D_MODEL)),
        "w_gate": w(ks[18], (L, D_MODEL, FFN_HIDDEN), D_MODEL),
        "w_up": w(ks[19], (L, D_MODEL, FFN_HIDDEN), D_MODEL),
        "w_down": w(ks[20], (L, FFN_HIDDEN, D_MODEL), FFN_HIDDEN),
    }


def reference(x, mem, norm_mix, w_in, diff_q_gain, diff_k_gain, diff_lambda, diff_subln,
              group_scale, w_out, norm_x, norm_mem, xq, xkv, xq_gain, xk_gain, xo,
              norm_ffn, w_gate, w_up, w_down):
    B, S, D = x.shape
    M = mem.shape[1]
    pos = jnp.arange(S)
    split_at = list(np.cumsum(IN_SIZES)[:-1])

    for l in range(DEPTH):
        lam_init = 0.8 - 0.6 * math.exp(-0.3 * l)

        h = rms_norm(x, norm_mix[l])
        proj = h @ w_in[l]
        rq, rk, rv, rg, dq, dk, dv = jnp.split(proj, split_at, axis=-1)

        rq = rq.reshape(B, S, RET_HEADS, RET_DK).transpose(0, 2, 1, 3)
        rk = rk.reshape(B, S, RET_HEADS, RET_DK).transpose(0, 2, 1, 3)
        rv = rv.reshape(B, S, RET_HEADS, RET_DV).transpose(0, 2, 1, 3)
        rq = rotary(rq, pos)
        rk = rotary(rk, pos) * (RET_DK ** -0.5)
        o_ret = rms_norm(retention_chunkwise(rq, rk, rv).astype(x.dtype))
        o_ret = o_ret.transpose(0, 2, 1, 3).reshape(B, S, RET_WIDTH) * jax.nn.silu(rg)

        dq = dq.reshape(B, S, DIFF_HEADS, 2, DIFF_DQK).transpose(0, 2, 3, 1, 4)
        dk = dk.reshape(B, S, DIFF_HEADS, 2, DIFF_DQK).transpose(0, 2, 3, 1, 4)
        dv = dv.reshape(B, S, DIFF_HEADS, DIFF_DV).transpose(0, 2, 1, 3)
        dq = rms_norm(dq, diff_q_gain[l])
        dk = rms_norm(dk, diff_k_gain[l])
        lv = diff_lambda[l].astype(jnp.float32)
        lam = jnp.exp(jnp.sum(lv[0] * lv[1])) - jnp.exp(jnp.sum(lv[2] * lv[3])) + lam_init
        o_diff = diff_attention_causal(dq, dk, dv, lam)
        o_diff = rms_norm(o_diff, diff_subln[l]) * (1.0 - lam_init)
        o_diff = o_diff.transpose(0, 2, 1, 3).reshape(B, S, DIFF_WIDTH)

        mix = jnp.concatenate([o_ret, o_diff], axis=-1) * group_scale[l]
        x = x + mix @ w_out[l]

        h = rms_norm(x, norm_x[l])
        m = rms_norm(mem, norm_mem[l])
        q = (h @ xq[l]).reshape(B, S, XATTN_HEADS, XATTN_DH)
        k, v = jnp.split(m @ xkv[l], 2, axis=-1)
        k = k.reshape(B, M, XATTN_HEADS, XATTN_DH)
        v = v.reshape(B, M, XATTN_HEADS, XATTN_DH)
        q = rms_norm(q, xq_gain[l])
        k = rms_norm(k, xk_gain[l])
        s = jnp.einsum('bshd,bmhd->bhsm', q, k).astype(jnp.float32) * (XATTN_DH ** -0.5)
        p = jax.nn.softmax(s, axis=-1).astype(v.dtype)
        o = jnp.einsum('bhsm,bmhd->bshd', p, v).reshape(B, S, D)
        x = x + o @ xo[l]

        h = rms_norm(x, norm_ffn[l])
        x = x + (jax.nn.silu(h @ w_gate[l]) * (h @ w_up[l])) @ w_down[l]

    return x
```

```python
import math
from contextlib import ExitStack

import numpy as np
import ml_dtypes

import concourse.bass as bass
import concourse.mybir as mybir
from concourse.bass_utils import run_bass_kernel_spmd

F32 = mybir.dt.float32
BF16 = mybir.dt.bfloat16
ALU = mybir.AluOpType
AF = mybir.ActivationFunctionType
AX = mybir.AxisListType

D_MODEL = 1024
SEQ = 8192
NT_ALL = SEQ // 128
NT_OWN = 16
EPS = 1e-6
FFN = 2816
NFC = FFN // 128

ENGS = ("pe", "act", "dve", "pool", "sp")
EPOCH = 30000
NSLOT = 8


class Op:
    __slots__ = ("eng", "fn", "reads", "writes", "is_dma", "idx", "seq", "marked", "waits", "cnt",
                 "dslot", "dval", "selfsync", "bar")

    def __init__(self, eng, fn, reads, writes, is_dma, selfsync, bar=False):
        self.eng = eng
        self.fn = fn
        self.reads = reads
        self.writes = writes
        self.is_dma = is_dma
        self.marked = False
        self.waits = []
        self.selfsync = selfsync
        self.bar = bar


class Prog:
    def __init__(self, nc, same_engine_sync=True):
        self.nc = nc
        self.ops = []
        self.same_engine_sync = same_engine_sync
        self.nbar = 0

    @staticmethod
    def _excl(reads, writes):
        r2, w2 = [], list(writes)
        for t in reads:
            n = t[0] if isinstance(t, tuple) else t
            if isinstance(n, str) and n.startswith("ps_"):
                w2.append(t)
            else:
                r2.append(t)
        return tuple(r2), tuple(w2)

    def op(self, eng, fn, reads=(), writes=(), selfsync=None):
        reads, writes = self._excl(reads, writes)
        o = Op(eng, fn, tuple(reads), tuple(writes), False,
               self.same_engine_sync if selfsync is None else selfsync)
        self.ops.append(o)
        return o

    def dma(self, eng, fn, reads=(), writes=()):
        o = Op(eng, fn, tuple(reads), tuple(writes), True, True)
        self.ops.append(o)
        return o

    def barrier(self):
        k = self.nbar
        self.nbar += 1
        for e in ENGS:
            self.ops.append(Op(e, lambda eng: eng.nop(), (), (("bar", k, e),), False, True, bar=True))
        for e in ENGS:
            self.ops.append(Op(e, lambda eng: eng.nop(), tuple(("bar", k, e2) for e2 in ENGS), (),
                               False, False))

    def build(self):
        ops = self.ops
        last_w = {}
        readers = {}
        per_eng = {e: [] for e in ENGS}
        dma_count = {e: 0 for e in ENGS}
        dma_ring = {e: [] for e in ENGS}
        pending_dma = {e: [] for e in ENGS}
        last_real = {e: None for e in ENGS}
        for i, o in enumerate(ops):
            o.idx = i
            o.seq = len(per_eng[o.eng])
            per_eng[o.eng].append(o)
            deps = set()
            for t in o.reads:
                if t in last_w:
                    deps.add(last_w[t])
            for t in o.writes:
                if t in last_w:
                    deps.add(last_w[t])
                for r in readers.get(t, ()):
                    deps.add(r)
            if o.is_dma:
                k = dma_count[o.eng]
                dma_count[o.eng] += 1
                o.dslot = k % NSLOT
                o.dval = 16 * (k // NSLOT + 1)
                if k >= NSLOT:
                    deps.add(dma_ring[o.eng][k - NSLOT].idx)
                dma_ring[o.eng].append(o)
                pending_dma[o.eng].append(i)
            if o.bar:
                for d in pending_dma[o.eng]:
                    deps.add(d)
                pending_dma[o.eng] = []
                if last_real[o.eng] is not None:
                    deps.add(last_real[o.eng])
            elif not o.is_dma:
                last_real[o.eng] = i
            deps.discard(i)
            o.waits = deps
            for t in o.reads:
                readers.setdefault(t, []).append(i)
            for t in o.writes:
                last_w[t] = i
                readers[t] = []
        known = {e: {} for e in ENGS}
        for o in ops:
            need_eng = {}
            need_dma = {}
            for d in o.waits:
                p = ops[d]
                if p.is_dma:
                    key = (p.eng, p.dslot)
                    if need_dma.get(key, 0) < p.dval:
                        need_dma[key] = p.dval
                else:
                    if p.eng == o.eng and not o.is_dma and not o.bar and (o.eng == "pe" or not o.selfsync):
                        continue
                    if need_eng.get(p.eng, -1) < p.seq:
                        need_eng[p.eng] = p.seq
            w = []
            kn = known[o.eng]
            for e, s_ in need_eng.items():
                if kn.get(e, -1) >= s_:
                    continue
                kn[e] = s_
                per_eng[e][s_].marked = True
                w.append(("eng", e, s_))
            for key, v in need_dma.items():
                if kn.get(key, 0) >= v:
                    continue
                kn[key] = v
                w.append(("dma", key, v))
            o.waits = w
        for e in ENGS:
            c = 0
            for o in per_eng[e]:
                if o.marked and not o.is_dma:
                    c += 1
                o.cnt = c
        return per_eng

    def emit(self):
        nc = self.nc
        per_eng = self.build()
        with ExitStack() as st:
            esems = {}
            for e in ENGS:
                n_mark = per_eng[e][-1].cnt if per_eng[e] else 0
                nep = n_mark // EPOCH + 1
                esems[e] = [st.enter_context(nc.semaphore(f"s_{e}_{k}")) for k in range(nep)]
            dsems = {}
            for e in ENGS:
                if any(o.is_dma for o in per_eng[e]):
                    dsems[e] = [st.enter_context(nc.semaphore(f"d_{e}_{k}")) for k in range(NSLOT)]
            block = st.enter_context(nc.Block())

            def sem_for(e, cnt):
                k = (cnt - 1) // EPOCH
                return esems[e][k], cnt - k * EPOCH

            def run_engine(ename, engobj):
                for o in per_eng[ename]:
                    for w in o.waits:
                        if w[0] == "eng":
                            _, e, s = w
                            sem, val = sem_for(e, per_eng[e][s].cnt)
                            engobj.wait_ge(sem, val)
                        else:
                            _, key, v = w
                            engobj.wait_ge(dsems[key[0]][key[1]], v)
                    ins = o.fn(engobj)
                    if o.is_dma:
                        ins.then_inc(dsems[ename][o.dslot], 16)
                    elif o.marked:
                        sem, val = sem_for(ename, o.cnt)
                        ins.then_inc(sem, 1)

            if per_eng["pe"]:
                block.tensor(lambda eng: run_engine("pe", eng))
            if per_eng["act"]:
                block.scalar(lambda eng: run_engine("act", eng))
            if per_eng["dve"]:
                block.vector(lambda eng: run_engine("dve", eng))
            if per_eng["pool"]:
                block.gpsimd(lambda eng: run_engine("pool", eng))
            if per_eng["sp"]:
                block.sync(lambda eng: run_engine("sp", eng))


BV_QG, BV_KG, BV_LAM, BV_SUBLN, BV_GSC, BV_XQG, BV_XKG, BV_N = 0, 64, 128, 384, 512, 1536, 1792, 2048
CR_COSA, CR_SINA, CR_COSO, CR_SINO, CR_DM, CR_ZETA, CR_XI, CR_GC, CR_SEL, CR_GM, CR_N = (
    0, 2048, 4096, 4608, 5120, 6144, 6152, 6160, 6672, 6676, 7188)


def _rot_tables(pos):
    half = 32
    inv = (np.float32(1.0) / (np.float32(10000.0) ** np.linspace(0.0, 1.0, half, dtype=np.float32))).astype(np.float32)
    ang = (pos.astype(np.float32)[:, None] * inv[None, :]).astype(np.float32)
    return np.cos(ang.astype(np.float64)).astype(np.float32), np.sin(ang.astype(np.float64)).astype(np.float32)


def _core_tables(j):
    H = 8
    C = 128
    log_g = np.log1p(-(2.0 ** (-5.0 - np.arange(H, dtype=np.float64))))
    idx = np.arange(C, dtype=np.float64)
    ct = np.zeros((128, CR_N), np.float32)
    cosa, sina = _rot_tables(np.arange(SEQ))
    ct[:, CR_COSA:CR_COSA + 2048] = cosa.reshape(64, 128, 32).transpose(1, 0, 2).reshape(128, 2048)
    ct[:, CR_SINA:CR_SINA + 2048] = sina.reshape(64, 128, 32).transpose(1, 0, 2).reshape(128, 2048)
    own_pos = ((4 * np.arange(16)[:, None] + j) * 128 + np.arange(128)[None, :]).reshape(-1)
    coso, sino = _rot_tables(own_pos)
    ct[:, CR_COSO:CR_COSO + 512] = coso.reshape(16, 128, 32).transpose(1, 0, 2).reshape(128, 512)
    ct[:, CR_SINO:CR_SINO + 512] = sino.reshape(16, 128, 32).transpose(1, 0, 2).reshape(128, 512)
    rel = idx[None, :] - idx[:, None]
    dm = np.zeros((128, 2, 4, 128), np.float64)
    for h in range(H):
        p, e = h // 2, h % 2
        dm[:, e, p, :] = np.where(rel >= 0, np.exp(log_g[h] * np.maximum(rel, 0.0)), 0.0)
    ct[:, CR_DM:CR_DM + 1024] = dm.reshape(128, 1024)
    ct[:, CR_ZETA:CR_ZETA + 8] = np.exp(log_g[None, :] * (C - 1.0 - idx[:, None]))
    ct[:, CR_XI:CR_XI + 8] = np.exp(log_g[None, :] * (idx[:, None] + 1.0))
    ct[:, CR_GC:CR_GC + 512] = np.repeat(np.exp(log_g * C), 64)[None, :]
    sel = np.zeros(4, np.float32)
    sel[j] = 1.0
    ct[:, CR_SEL:CR_SEL + 4] = sel[None, :]
    gm = np.zeros((128, 8, 64), np.float32)
    for h in range(H):
        e = h % 2
        gm[e * 64:(e + 1) * 64, h, :] = 1.0
    ct[:, CR_GM:CR_GM + 512] = gm.reshape(128, 512)
    am = np.zeros((128, 4, 128), np.float32)
    kk = np.arange(128)[:, None]
    qq = np.arange(128)[None, :]
    for r in range(4):
        if r < j:
            am[:, r, :] = 1.0
        elif r == j:
            am[:, r, :] = (kk <= qq).astype(np.float32)
    return ct, am.reshape(128, 512)


def build_nc(debug=False, stop_after=None):
    nc = bass.Bass("TRN2", target_bir_lowering=False)

    def din(name, shape, dt=F32):
        return nc.dram_tensor(name, list(shape), dt, kind="ExternalInput").ap()

    xT_all = din("xT_all", [1024, SEQ])
    xT_own = din("xT_own", [1024, 2048])
    x_own = din("x_own", [2048, 1024])
    memT = din("memT", [1024, 256])
    w_in = din("w_in", [1024, 3584])
    w_out = din("w_out", [1024, 1024])
    xq_w = din("xq", [1024, 1024])
    xkv_w = din("xkv", [1024, 2048])
    xo_w = din("xo", [1024, 1024])
    w_gate = din("w_gate", [1024, FFN])
    w_up = din("w_up", [1024, FFN])
    w_down = din("w_down", [FFN, 1024])
    gammas_d = din("gammas", [128, 32])
    bvec_d = din("bvec", [128, BV_N])
    ctabR_d = din("ctabR", [128, CR_N])
    amask_d = din("amask", [128, 512])
    ident_d = din("ident", [128, 128])
    out_d = nc.dram_tensor("out", [2048, 1024], F32, kind="ExternalOutput").ap()
    x2s_d = nc.dram_tensor("x2s", [2048, 1024], F32, kind="ExternalOutput" if debug else "Internal").ap()
    if debug:
        dbg_mix = nc.dram_tensor("dbg_mix", [128, 8 * 2048], BF16, kind="ExternalOutput").ap()

    P = Prog(nc)
    uid = [0]

    def T(*a):
        return tuple(a)

    with ExitStack() as gst:
        def gsb(name, shape, dt):
            return gst.enter_context(nc.sbuf_tensor("g_" + name, list(shape), dt))

        ident_f = gsb("ident_f", [128, 128], F32)
        ident = gsb("ident", [128, 128], BF16)
        gam = gsb("gam", [128, 32], F32)
        bvec = gsb("bvec", [128, BV_N], F32)
        mhalf = gsb("mhalf", [128, 8], F32)
        ones_bf = gsb("ones_bf", [128, 2], BF16)
        wst = [gsb(f"wst{i}", [128, 1024], F32) for i in range(2)]
        neglam = gsb("neglam", [128, 1], F32)
        Gd = gsb("Gd", [128, 512], F32)
        lamtmp = gsb("lamtmp", [128, 128], F32)
        lam2 = gsb("lam2", [128, 4], F32)
        mst = ExitStack()
        mixT = mst.enter_context(nc.sbuf_tensor("g_mixT", [128, 8, 2048], BF16))

        P.dma("sp", lambda e: e.dma_start(out=ident_f[:], in_=ident_d), writes=["ident_f"])
        P.dma("sp", lambda e: e.dma_start(out=gam[:], in_=gammas_d), writes=["gam"])
        P.dma("sp", lambda e: e.dma_start(out=bvec[:], in_=bvec_d), writes=["bvec"])
        P.op("dve", lambda e: e.tensor_copy(out=ident[:], in_=ident_f[:]), reads=["ident_f"], writes=["ident"])
        P.op("pool", lambda e: e.memset(mhalf[:], -0.5), writes=["mhalf"])
        if debug:
            P.op("pool", lambda e: e.memset(mixT[:], 0.0), writes=[T("mixT", t) for t in range(16)])
        P.op("pool", lambda e: e.memset(ones_bf[:], 1.0), writes=["ones"])
        lv = bvec[:, BV_LAM:BV_LAM + 256].rearrange("p (a b d) -> p a b d", a=2, b=2)
        P.op("dve", lambda e: e.tensor_tensor(out=lamtmp[:].rearrange("p (a d) -> p a d", a=2),
                                              in0=lv[:, :, 0, :], in1=lv[:, :, 1, :], op=ALU.mult),
             reads=["bvec"], writes=["lamtmp"])
        P.op("dve", lambda e: e.tensor_reduce(out=lam2[:, 0:2], in_=lamtmp[:].rearrange("p (a d) -> p a d", a=2),
                                              axis=AX.X, op=ALU.add), reads=["lamtmp"], writes=["lam2a"])
        P.op("act", lambda e: e.activation(out=lam2[:, 2:4], in_=lam2[:, 0:2], func=AF.Exp),
             reads=["lam2a"], writes=["lam2b"])
        P.op("dve", lambda e: e.tensor_tensor(out=neglam[:], in0=lam2[:, 3:4], in1=lam2[:, 2:3], op=ALU.subtract),
             reads=["lam2b"], writes=["neglam"])
        P.op("dve", lambda e: e.tensor_scalar(out=neglam[:], in0=neglam[:], scalar1=-0.2, scalar2=None, op0=ALU.add),
             reads=["neglam"], writes=["neglam"])
        P.op("dve", lambda e: e.scalar_tensor_tensor(
            out=Gd[:].rearrange("p (h d) -> p h d", h=4),
            in0=bvec[:, BV_GSC + 512:BV_GSC + 1024].rearrange("p (h d) -> p h d", h=4), scalar=0.8,
            in1=bvec[:, BV_SUBLN:BV_SUBLN + 128].unsqueeze(1).to_broadcast([128, 4, 128]),
            op0=ALU.mult, op1=ALU.mult), reads=["bvec"], writes=["Gd"])

        wl_count = [0]

        def load_weight(dst, src, kc_list, c0, ncols, gcol=None, dst_c0=0, tok=None):
            for i, kc in enumerate(kc_list):
                for cs in range(0, ncols, 1024):
                    cw = min(1024, ncols - cs)
                    k = wl_count[0]
                    wl_count[0] += 1
                    stg = wst[k % 2]
                    stok = T("wst", k % 2)
                    P.dma("sp", lambda e, stg=stg, kc=kc, cs=cs, cw=cw: e.dma_start(
                        out=stg[:, 0:cw], in_=src[kc * 128:(kc + 1) * 128, c0 + cs:c0 + cs + cw]), writes=[stok])
                    eng = "pool" if k % 2 == 0 else "dve"
                    d_ap = dst[:, i, dst_c0 + cs:dst_c0 + cs + cw]
                    if gcol is None:
                        P.op(eng, lambda e, stg=stg, cw=cw, d_ap=d_ap: e.tensor_copy(out=d_ap, in_=stg[:, 0:cw]),
                             reads=[stok], writes=[tok])
                    else:
                        P.op(eng, lambda e, stg=stg, cw=cw, d_ap=d_ap, kc=kc: e.tensor_scalar(
                            out=d_ap, in0=stg[:, 0:cw], scalar1=gam[:, gcol + kc:gcol + kc + 1], scalar2=None,
                            op0=ALU.mult), reads=[stok, "gam"], writes=[tok])

        def rstd_chain(ss_ap, n, out_ap, tmp_ap, k, rd, wr, tmp_tok):
            P.op("dve", lambda e: e.tensor_scalar(out=tmp_ap, in0=ss_ap, scalar1=1.0 / n, scalar2=EPS,
                                                  op0=ALU.mult, op1=ALU.add), reads=rd, writes=[tmp_tok])
            P.op("pool", lambda e: e.tensor_tensor(out=out_ap, in0=tmp_ap, in1=mhalf[:, 0:k], op=ALU.pow),
                 reads=[tmp_tok, "mhalf"], writes=wr)

        def rotary(src, dst, cos_ap, sin_ap, tmp, rd, wr, key):
            s4 = src.rearrange("p (h two d) -> p h two d", h=8, two=2)
            d4 = dst.rearrange("p (h two d) -> p h two d", h=8, two=2)
            x1, x2 = s4[:, :, 0, :], s4[:, :, 1, :]
            cb = cos_ap.unsqueeze(1).to_broadcast([128, 8, 32])
            sb_ = sin_ap.unsqueeze(1).to_broadcast([128, 8, 32])
            t = [tmp[:, i, :].rearrange("p (h d) -> p h d", h=8) for i in range(4)]
            tk = [T("rot", key, i) for i in range(4)]
            P.op("dve", lambda e: e.tensor_tensor(out=t[0], in0=x1, in1=cb, op=ALU.mult), reads=rd, writes=[tk[0]])
            P.op("dve", lambda e: e.tensor_tensor(out=t[1], in0=x2, in1=sb_, op=ALU.mult), reads=rd, writes=[tk[1]])
            P.op("dve", lambda e: e.tensor_tensor(out=d4[:, :, 0, :], in0=t[0], in1=t[1], op=ALU.subtract),
                 reads=[tk[0], tk[1]], writes=wr)
            P.op("pool", lambda e: e.tensor_tensor(out=t[2], in0=x2, in1=cb, op=ALU.mult), reads=rd, writes=[tk[2]])
            P.op("pool", lambda e: e.tensor_tensor(out=t[3], in0=x1, in1=sb_, op=ALU.mult), reads=rd, writes=[tk[3]])
            P.op("pool", lambda e: e.tensor_tensor(out=d4[:, :, 1, :], in0=t[2], in1=t[3], op=ALU.add),
                 reads=[tk[2], tk[3]], writes=wr)

        def phase_R():
            with ExitStack() as st:
                PH = "R"

                def sb(name, shape, dt):
                    return st.enter_context(nc.sbuf_tensor(PH + "_" + name, list(shape), dt))

                def ps(name, shape, dt):
                    return st.enter_context(nc.psum_tensor(PH + "_" + name, list(shape), dt))

                WR = sb("WR", [128, 8, 2048], BF16)
                ctab = sb("ctabR", [128, CR_N], F32)
                xs = [sb(f"xs{i}", [128, 8, 256], F32) for i in range(2)]
                xso = [sb(f"xso{i}", [128, 8, 128], F32) for i in range(2)]
                xb = [sb(f"xb{i}", [128, 8, 128], BF16) for i in range(2)]
                sq = [sb(f"sq{i}", [128, 8, 128], BF16) for i in range(2)]
                rs = [sb(f"rs{i}", [128, 4], F32) for i in range(2)]
                ksc = [sb(f"ksc{i}", [128, 512], F32) for i in range(2)]
                rtmp = [sb(f"rtmp{i}", [128, 4, 256], F32) for i in range(2)]
                krot = [sb(f"krot{i}", [128, 512], BF16) for i in range(2)]
                zr = [sb(f"zr{i}", [128, 8], F32) for i in range(2)]
                vz = [sb(f"vz{i}", [128, 512], BF16) for i in range(2)]
                Rst = sb("Rst", [128, 512], F32)
                Rtmp = sb("Rtmp", [128, 512], F32)
                Rsel = sb("Rsel", [128, 512], F32)
                Rselb = sb("Rselb", [128, 512], BF16)
                qsc = sb("qsc", [128, 512], F32)
                qrot = sb("qrot", [128, 512], BF16)
                krot_o = sb("krot_o", [128, 512], BF16)
                v_o = sb("v_o", [128, 512], BF16)
                sg = sb("sg", [128, 512], F32)
                qT = sb("qT", [128, 4, 128], BF16)
                kT = sb("kT", [128, 4, 128], BF16)
                sDT = sb("sDT", [128, 2, 4, 128], BF16)
                o_sb = sb("o_sb", [128, 512], F32)
                o_t1 = sb("o_t1", [128, 512], F32)
                o_t2 = sb("o_t2", [128, 512], F32)
                gn = sb("gn", [128, 24], F32)
                mixr = sb("mixr", [128, 512], BF16)

                pA = ps("pA", [128, 7, 512], F32)
                pT = ps("ps_T", [128, 1024], BF16)

                P.dma("sp", lambda e: e.dma_start(out=ctab[:], in_=ctabR_d), writes=["ctab"])
                load_weight(WR, w_in, list(range(8)), 0, 2048, gcol=0, tok="WR")
                P.op("dve", lambda e: e.memset(Rst[:], 0.0), writes=["Rst"])

                xTa = xT_all.rearrange("(kc p) n -> p kc n", p=128)
                xTo = xT_own.rearrange("(kc p) n -> p kc n", p=128)

                def tile_front(src_buf, src_tok, col0, slot, bank_ss, key):
                    xbt, sqt, rst = xb[slot], sq[slot], rs[slot]
                    P.op("dve", lambda e: e.tensor_copy(out=xbt[:], in_=src_buf[:, :, col0:col0 + 128]),
                         reads=[src_tok], writes=[T("xb", slot)])
                    P.op("act", lambda e: e.activation(out=sqt[:], in_=src_buf[:, :, col0:col0 + 128], func=AF.Square),
                         reads=[src_tok], writes=[T("sq", slot)])
                    ssp = pA[:, bank_ss, 0:1]
                    for kc in range(8):
                        P.op("pe", lambda e, kc=kc: e.matmul(ssp, lhsT=sqt[:, kc, :], rhs=ones_bf[:, 0:1],
                                                             start=(kc == 0), stop=(kc == 7)),
                             reads=[T("sq", slot), "ones"], writes=[T("ps_A", bank_ss)])
                    rstd_chain(ssp, 1024.0, rst[:, 1:2], rst[:, 0:1], 1, [T("ps_A", bank_ss)], [T("rs", slot)],
                               T("rstmp", slot))
                    P.op("dve", lambda e: e.tensor_scalar(out=rst[:, 2:3], in0=rst[:, 1:2], scalar1=0.125, scalar2=None,
                                                          op0=ALU.mult), reads=[T("rs", slot)], writes=[T("rs8", slot)])

                def proj(slot, bank, c0, ncols, W, wtok):
                    for kc in range(8):
                        P.op("pe", lambda e, kc=kc: e.matmul(pA[:, bank, 0:ncols], lhsT=xb[slot][:, kc, :],
                                                             rhs=W[:, kc, c0:c0 + ncols], start=(kc == 0), stop=(kc == 7)),
                             reads=[T("xb", slot), wtok], writes=[T("ps_A", bank)])

                cosa = ctab[:, CR_COSA:CR_COSA + 2048].rearrange("p (n d) -> p n d", n=64)
                sina = ctab[:, CR_SINA:CR_SINA + 2048].rearrange("p (n d) -> p n d", n=64)
                coso = ctab[:, CR_COSO:CR_COSO + 512].rearrange("p (n d) -> p n d", n=16)
                sino = ctab[:, CR_SINO:CR_SINO + 512].rearrange("p (n d) -> p n d", n=16)
                dmT = ctab[:, CR_DM:CR_DM + 1024]
                zeta = ctab[:, CR_ZETA:CR_ZETA + 8]
                xi = ctab[:, CR_XI:CR_XI + 8]
                gC = ctab[:, CR_GC:CR_GC + 512]
                selc = ctab[:, CR_SEL:CR_SEL + 4]
                gmask = ctab[:, CR_GM:CR_GM + 512]

                for gi in range(16):
                    xob = xso[gi % 2]
                    P.dma("sp", lambda e, xob=xob, gi=gi: e.dma_start(out=xob[:], in_=xTo[:, :, gi * 128:(gi + 1) * 128]),
                          writes=[T("xso", gi % 2)])
                    for tt in range(4):
                        n = gi * 4 + tt
                        sl = n % 2
                        hf = tt // 2
                        xsb = xs[hf]
                        if tt % 2 == 0:
                            P.dma("sp", lambda e, xsb=xsb, gi=gi, hf=hf: e.dma_start(
                                out=xsb[:], in_=xTa[:, :, gi * 512 + hf * 256:gi * 512 + (hf + 1) * 256]),
                                writes=[T("xs", hf)])
                        tile_front(xsb, T("xs", hf), (tt % 2) * 128, sl, 3, n)
                        proj(sl, 0, 512, 512, WR, "WR")
                        proj(sl, 1, 1024, 512, WR, "WR")
                        P.op("act", lambda e, sl=sl: e.activation(out=ksc[sl][:], in_=pA[:, 0, :], func=AF.Copy,
                                                                  scale=rs[sl][:, 2:3]),
                             reads=[T("ps_A", 0), T("rs8", sl)], writes=[T("ksc", sl)])
                        rotary(ksc[sl][:], krot[sl][:], cosa[:, n, :], sina[:, n, :], rtmp[sl],
                               [T("ksc", sl), "ctab"], [T("krot", sl)], sl)
                        P.op("dve", lambda e, sl=sl: e.tensor_scalar(out=zr[sl][:], in0=zeta, scalar1=rs[sl][:, 1:2],
                                                                     scalar2=None, op0=ALU.mult),
                             reads=["ctab", T("rs", sl)], writes=[T("zr", sl)])
                        P.op("dve", lambda e, sl=sl: e.tensor_tensor(
                            out=vz[sl][:].rearrange("p (h d) -> p h d", h=8),
                            in0=pA[:, 1, :].rearrange("p (h d) -> p h d", h=8),
                            in1=zr[sl][:].unsqueeze(2).to_broadcast([128, 8, 64]), op=ALU.mult),
                             reads=[T("ps_A", 1), T("zr", sl)], writes=[T("vz", sl)])
                        for h in range(8):
                            p_ = h // 2
                            P.op("pe", lambda e, sl=sl, h=h, p_=p_: e.matmul(
                                pA[:, 2, h * 64:(h + 1) * 64], lhsT=krot[sl][:, p_ * 128:(p_ + 1) * 128],
                                rhs=vz[sl][:, h * 64:(h + 1) * 64], start=True, stop=True),
                                 reads=[T("krot", sl), T("vz", sl)], writes=[T("ps_A", 2)])
                        r = tt
                        if r == 0:
                            P.op("dve", lambda e, r=r: e.tensor_scalar(out=Rsel[:], in0=Rst[:], scalar1=selc[:, r:r + 1],
                                                                       scalar2=None, op0=ALU.mult),
                                 reads=["Rst", "ctab"], writes=["Rsel"])
                        else:
                            P.op("dve", lambda e, r=r: e.scalar_tensor_tensor(out=Rsel[:], in0=Rst[:], scalar=selc[:, r:r + 1],
                                                                              in1=Rsel[:], op0=ALU.mult, op1=ALU.add),
                                 reads=["Rst", "ctab", "Rsel"], writes=["Rsel"])
                        P.op("dve", lambda e: e.tensor_tensor(out=Rtmp[:], in0=Rst[:], in1=gC, op=ALU.mult),
                             reads=["Rst", "ctab"], writes=["Rtmp"])
                        P.op("dve", lambda e: e.tensor_tensor(out=Rst[:], in0=Rtmp[:], in1=pA[:, 2, :], op=ALU.add),
                             reads=["Rtmp", T("ps_A", 2)], writes=["Rst"])
                    t = gi
                    xob = xso[t % 2]
                    sl = 0
                    tile_front(xob, T("xso", t % 2), 0, sl, 3, 1000 + t)
                    proj(sl, 3, 0, 512, WR, "WR")
                    proj(sl, 4, 512, 512, WR, "WR")
                    proj(sl, 5, 1024, 512, WR, "WR")
                    proj(sl, 6, 1536, 512, WR, "WR")
                    P.op("act", lambda e: e.activation(out=qsc[:], in_=pA[:, 3, :], func=AF.Copy, scale=rs[0][:, 1:2]),
                         reads=[T("ps_A", 3), T("rs", 0)], writes=["qsc"])
                    rotary(qsc[:], qrot[:], coso[:, t, :], sino[:, t, :], rtmp[0], ["qsc", "ctab"], ["qrot"], 0)
                    P.op("act", lambda e: e.activation(out=ksc[0][:], in_=pA[:, 4, :], func=AF.Copy, scale=rs[0][:, 2:3]),
                         reads=[T("ps_A", 4), T("rs8", 0)], writes=[T("ksc", 0)])
                    rotary(ksc[0][:], krot_o[:], coso[:, t, :], sino[:, t, :], rtmp[1], [T("ksc", 0), "ctab"], ["krot_o"], 1)
                    P.op("act", lambda e: e.activation(out=v_o[:], in_=pA[:, 5, :], func=AF.Copy, scale=rs[0][:, 1:2]),
                         reads=[T("ps_A", 5), T("rs", 0)], writes=["v_o"])
                    P.op("act", lambda e: e.activation(out=sg[:], in_=pA[:, 6, :], func=AF.Silu, scale=rs[0][:, 1:2]),
                         reads=[T("ps_A", 6), T("rs", 0)], writes=["sg"])
                    for p_ in range(4):
                        P.op("pe", lambda e, p_=p_: e.transpose(pT[:, p_ * 128:(p_ + 1) * 128], qrot[:, p_ * 128:(p_ + 1) * 128],
                                                                 ident[:]), reads=["qrot", "ident"], writes=["ps_T"])
                    for p_ in range(4):
                        P.op("pe", lambda e, p_=p_: e.transpose(pT[:, 512 + p_ * 128:512 + (p_ + 1) * 128],
                                                                 krot_o[:, p_ * 128:(p_ + 1) * 128], ident[:]),
                             reads=["krot_o", "ident"], writes=["ps_T"])
                    P.op("act", lambda e: e.copy(out=qT[:].rearrange("p a b -> p (a b)"), in_=pT[:, 0:512]),
                         reads=["ps_T"], writes=["qT"])
                    P.op("dve", lambda e: e.tensor_copy(out=kT[:].rearrange("p a b -> p (a b)"), in_=pT[:, 512:1024]),
                         reads=["ps_T"], writes=["kT"])
                    for p_ in range(4):
                        for e_ in range(2):
                            P.op("pe", lambda e, p_=p_, e_=e_: e.matmul(
                                pA[:, e_, p_ * 128:(p_ + 1) * 128], lhsT=kT[e_ * 64:(e_ + 1) * 64, p_, :],
                                rhs=qT[e_ * 64:(e_ + 1) * 64, p_, :], start=True, stop=True),
                                 reads=["qT", "kT"], writes=[T("ps_A", e_)])
                    P.op("dve", lambda e: e.tensor_tensor(out=sDT[:].rearrange("p a b c -> p (a b c)"),
                                                          in0=pA[:, 0:2, :].rearrange("p a b -> p (a b)"), in1=dmT,
                                                          op=ALU.mult),
                         reads=[T("ps_A", 0), T("ps_A", 1), "ctab"], writes=["sDT"])
                    P.op("pool", lambda e: e.tensor_tensor(out=Rselb[:], in0=Rsel[:], in1=gmask, op=ALU.mult),
                         reads=["Rsel", "ctab"], writes=["Rselb"])
                    for h in range(8):
                        p_, e_ = h // 2, h % 2
                        P.op("pe", lambda e, h=h, p_=p_, e_=e_: e.matmul(
                            pA[:, 4, h * 64:(h + 1) * 64], lhsT=sDT[:, e_, p_, :], rhs=v_o[:, h * 64:(h + 1) * 64],
                            start=True, stop=True), reads=["sDT", "v_o"], writes=[T("ps_A", 4)])
                    for h in range(8):
                        p_ = h // 2
                        P.op("pe", lambda e, h=h, p_=p_: e.matmul(
                            pA[:, 5, h * 64:(h + 1) * 64], lhsT=qT[:, p_, :], rhs=Rselb[:, h * 64:(h + 1) * 64],
                            start=True, stop=True), reads=["qT", "Rselb"], writes=[T("ps_A", 5)])
                    P.op("dve", lambda e: e.tensor_tensor(out=o_t1[:].rearrange("p (h d) -> p h d", h=8),
                                                          in0=pA[:, 5, :].rearrange("p (h d) -> p h d", h=8),
                                                          in1=xi.unsqueeze(2).to_broadcast([128, 8, 64]), op=ALU.mult),
                         reads=[T("ps_A", 5), "ctab"], writes=["o_t1"])
                    P.op("dve", lambda e: e.tensor_tensor(out=o_sb[:], in0=o_t1[:], in1=pA[:, 4, :], op=ALU.add),
                         reads=["o_t1", T("ps_A", 4)], writes=["o_sb"])
                    P.op("pool", lambda e: e.tensor_tensor(out=o_t2[:], in0=o_sb[:], in1=o_sb[:], op=ALU.mult),
                         reads=["o_sb"], writes=["o_t2"])
                    P.op("dve", lambda e: e.tensor_reduce(out=gn[:, 0:8], in_=o_t2[:].rearrange("p (h d) -> p h d", h=8),
                                                          axis=AX.X, op=ALU.add), reads=["o_t2"], writes=["gn_ss"])
                    rstd_chain(gn[:, 0:8], 64.0, gn[:, 16:24], gn[:, 8:16], 8, ["gn_ss"], ["gn_rs"], "gn_tmp")
                    P.op("dve", lambda e: e.tensor_tensor(out=o_t1[:].rearrange("p (h d) -> p h d", h=8),
                                                          in0=o_sb[:].rearrange("p (h d) -> p h d", h=8),
                                                          in1=gn[:, 16:24].unsqueeze(2).to_broadcast([128, 8, 64]),
                                                          op=ALU.mult), reads=["o_sb", "gn_rs"], writes=["o_t1"])
                    P.op("pool", lambda e: e.tensor_tensor(out=o_t2[:], in0=o_t1[:], in1=sg[:], op=ALU.mult),
                         reads=["o_t1", "sg"], writes=["o_t2"])
                    P.op("dve", lambda e: e.tensor_tensor(out=mixr[:], in0=o_t2[:], in1=bvec[:, BV_GSC:BV_GSC + 512],
                                                          op=ALU.mult), reads=["o_t2", "bvec"], writes=["mixr"])
                    for c in range(4):
                        P.op("pe", lambda e, c=c: e.transpose(pT[:, c * 128:(c + 1) * 128], mixr[:, c * 128:(c + 1) * 128],
                                                               ident[:]), reads=["mixr", "ident"], writes=["ps_T"])
                    P.op("act", lambda e, t=t: e.copy(out=mixT[:, 0:4, t * 128:(t + 1) * 128],
                                                      in_=pT[:, 0:512].rearrange("p (c n) -> p c n", c=4)),
                         reads=["ps_T"], writes=[T("mixT", t)])
                P.barrier()
        phase_R()

        import os as _os

        def front(xbt, sqt, rst, src_ap, src_tok, slot, ss_ap, ss_tok):
            P.op("dve", lambda e: e.tensor_copy(out=xbt[:], in_=src_ap), reads=[src_tok], writes=[T("xb", slot)])
            P.op("act", lambda e: e.activation(out=sqt[:], in_=src_ap, func=AF.Square),
                 reads=[src_tok], writes=[T("sq", slot)])
            for kc in range(8):
                P.op("pe", lambda e, kc=kc: e.matmul(ss_ap, lhsT=sqt[:, kc, :], rhs=ones_bf[:, 0:1],
                                                     start=(kc == 0), stop=(kc == 7)),
                     reads=[T("sq", slot), "ones"], writes=[ss_tok])
            rstd_chain(ss_ap, 1024.0, rst[:, 1:2], rst[:, 0:1], 1, [ss_tok], [T("rs", slot)], T("rstmp", slot))

        def headnorm_T(src_ps, ps_tok, rst_ap, rs_tok, gain_ap, ng, dsz, bufs, key, pT, dst_ap, dst_tok):
            y, ysq, nst, ynb = bufs
            n = ng * dsz
            P.op("act", lambda e: e.activation(out=y[:, 0:n], in_=src_ps, func=AF.Copy, scale=rst_ap),
                 reads=[ps_tok, rs_tok], writes=[T("hn_y", key)])
            P.op("pool", lambda e: e.tensor_tensor(out=ysq[:, 0:n], in0=y[:, 0:n], in1=y[:, 0:n], op=ALU.mult),
                 reads=[T("hn_y", key)], writes=[T("hn_ysq", key)])
            P.op("dve", lambda e: e.tensor_reduce(out=nst[:, 0:ng], in_=ysq[:, 0:n].rearrange("p (g d) -> p g d", g=ng),
                                                  axis=AX.X, op=ALU.add), reads=[T("hn_ysq", key)], writes=[T("hn_ss", key)])
            rstd_chain(nst[:, 0:ng], float(dsz), nst[:, 8:8 + ng], nst[:, 4:4 + ng], ng, [T("hn_ss", key)],
                       [T("hn_r", key)], T("hn_tmp", key))
            P.op("dve", lambda e: e.tensor_tensor(out=ysq[:, 0:n].rearrange("p (g d) -> p g d", g=ng),
                                                  in0=y[:, 0:n].rearrange("p (g d) -> p g d", g=ng),
                                                  in1=nst[:, 8:8 + ng].unsqueeze(2).to_broadcast([128, ng, dsz]),
                                                  op=ALU.mult), reads=[T("hn_y", key), T("hn_r", key), T("hn_ysq", key)],
                 writes=[T("hn_ysq", key)])
            P.op("pool", lambda e: e.tensor_tensor(out=ynb[:, 0:n].rearrange("p (g d) -> p g d", g=ng),
                                                   in0=ysq[:, 0:n].rearrange("p (g d) -> p g d", g=ng),
                                                   in1=gain_ap.unsqueeze(1).to_broadcast([128, ng, dsz]), op=ALU.mult),
                 reads=[T("hn_ysq", key), "bvec"], writes=[T("hn_ynb", key)])
            nch = n // 128
            for c in range(nch):
                P.op("pe", lambda e, c=c: e.transpose(pT[:, c * 128:(c + 1) * 128], ynb[:, c * 128:(c + 1) * 128], ident[:]),
                     reads=[T("hn_ynb", key), "ident"], writes=["ps_T"])
            P.op("act", lambda e: e.copy(out=dst_ap, in_=pT[:, 0:n].rearrange("p (c n) -> p c n", c=nch)),
                 reads=["ps_T"], writes=[dst_tok])

        NQT = int(_os.environ.get("KNQT", "16")) if debug else 16
        def phase_D(hp):
            with ExitStack() as st:
                PH = f"D{hp}"

                def sb(name, shape, dt):
                    return st.enter_context(nc.sbuf_tensor(PH + "_" + name, list(shape), dt))

                def ps(name, shape, dt):
                    return st.enter_context(nc.psum_tensor(PH + "_" + name, list(shape), dt))

                WD = sb("WD", [128, 8, 768], BF16)
                KT = sb("KT", [128, 2, SEQ], BF16)
                Va = sb("Va", [128, 64, 2, 130], BF16)
                QT = sb("QT", [128, 2, 2048], BF16)
                am = sb("am", [128, 512], F32)
                xs = [sb(f"xs{i}", [128, 8, 256], F32) for i in range(2)]
                xso = [sb(f"xso{i}", [128, 8, 128], F32) for i in range(2)]
                xb = [sb(f"xb{i}", [128, 8, 128], BF16) for i in range(2)]
                sq = [sb(f"sq{i}", [128, 8, 128], BF16) for i in range(2)]
                rs = [sb(f"rs{i}", [128, 4], F32) for i in range(2)]
                hb = [(sb(f"hy{i}", [128, 256], F32), sb(f"hysq{i}", [128, 256], F32), sb(f"hnst{i}", [128, 12], F32),
                       sb(f"hynb{i}", [128, 256], BF16)) for i in range(2)]
                Pm = [[sb(f"Pm{b_}{m_}", [128, 512], BF16) for m_ in range(2)] for b_ in range(2)]
                rc = sb("rc", [128, 4], F32)
                a0 = sb("a0", [128, 128], F32)
                od = sb("od", [128, 128], F32)
                osq = sb("osq", [128, 128], F32)
                pst = sb("pst", [128, 4], F32)
                odb = sb("odb", [128, 128], BF16)
                pP = ps("pP", [128, 512], F32)
                pS = [[ps(f"pS{b_}{m_}", [128, 512], F32) for m_ in range(2)] for b_ in range(2)]
                pO = [ps(f"pO{m_}", [128, 512], F32) for m_ in range(2)]
                pT = ps("pT", [128, 1024], BF16)

                P.dma("sp", lambda e, am=am: e.dma_start(out=am[:], in_=amask_d), writes=["am"])
                load_weight(WD, w_in, list(range(8)), 2048 + hp * 256, 256, gcol=0, dst_c0=0, tok="WD")
                load_weight(WD, w_in, list(range(8)), 2560 + hp * 256, 256, gcol=0, dst_c0=256, tok="WD")
                load_weight(WD, w_in, list(range(8)), 3072 + hp * 256, 256, gcol=0, dst_c0=512, tok="WD")
                P.op("pool", lambda e, Va=Va: e.memset(Va[:, :, :, 128:130], 1.0), writes=[T("Va", n) for n in range(64)])

                ob_cnt = [0]

                def attention(t, hh):
                    h = 2 * hp + hh
                    ob = ob_cnt[0] % 2
                    ob_cnt[0] += 1

                    def qk(G, b_):
                        for r in range(4):
                            kb = 4 * G + r
                            for mp in range(2):
                                P.op("pe", lambda e, r=r, kb=kb, mp=mp, b_=b_: e.matmul(
                                    pS[b_][mp][:, r * 128:(r + 1) * 128],
                                    lhsT=KT[mp * 64:(mp + 1) * 64, hh, kb * 128:(kb + 1) * 128],
                                    rhs=QT[mp * 64:(mp + 1) * 64, hh, t * 128:(t + 1) * 128], start=True, stop=True),
                                     reads=[T("KT", kb), T("QT", t)], writes=[T("ps_S", b_, mp)])

                    def ex(G, b_):
                        for mp in range(2):
                            P.op("act", lambda e, mp=mp, b_=b_: e.activation(out=Pm[b_][mp][:], in_=pS[b_][mp][:],
                                                                             func=AF.Exp, scale=0.125),
                                 reads=[T("ps_S", b_, mp)], writes=[T("Pm", b_, mp)])
                            if G == t:
                                P.op("dve", lambda e, mp=mp, b_=b_: e.tensor_tensor(out=Pm[b_][mp][:], in0=Pm[b_][mp][:],
                                                                                    in1=am[:], op=ALU.mult),
                                     reads=[T("Pm", b_, mp), "am"], writes=[T("Pm", b_, mp)])

                    def pv(G, b_):
                        for mp in range(2):
                            for r in range(4):
                                kb = 4 * G + r
                                P.op("pe", lambda e, r=r, kb=kb, mp=mp, b_=b_, G=G: e.matmul(
                                    pO[mp][:, 0:129], lhsT=Pm[b_][mp][:, r * 128:(r + 1) * 128],
                                    rhs=Va[:, kb, hh, 0:129], start=(G == 0 and r == 0), stop=(G == t and r == 3)),
                                     reads=[T("Pm", b_, mp), T("Va", kb)], writes=[T("ps_O", mp)])

                    qk(0, 0)
                    ex(0, 0)
                    for G in range(t + 1):
                        if G < t:
                            qk(G + 1, (G + 1) % 2)
                            ex(G + 1, (G + 1) % 2)
                        pv(G, G % 2)
                    for mp in range(2):
                        P.op("dve", lambda e, mp=mp: e.reciprocal(out=rc[:, mp:mp + 1], in_=pO[mp][:, 128:129]),
                             reads=[T("ps_O", mp)], writes=[T("rc", mp)])
                    P.op("act", lambda e: e.activation(out=a0[:], in_=pO[0][:, 0:128], func=AF.Copy, scale=rc[:, 0:1]),
                         reads=[T("ps_O", 0), T("rc", 0)], writes=["a0"])
                    P.op("dve", lambda e: e.tensor_tensor(out=rc[:, 2:3], in0=rc[:, 1:2], in1=neglam[:], op=ALU.mult),
                         reads=[T("rc", 1), "neglam"], writes=["rc_nl"])
                    P.op("dve", lambda e: e.scalar_tensor_tensor(out=od[:], in0=pO[1][:, 0:128], scalar=rc[:, 2:3],
                                                                 in1=a0[:], op0=ALU.mult, op1=ALU.add),
                         reads=[T("ps_O", 1), "rc_nl", "a0"], writes=["od"])
                    P.op("pool", lambda e: e.tensor_tensor(out=osq[:], in0=od[:], in1=od[:], op=ALU.mult),
                         reads=["od"], writes=["osq"])
                    P.op("dve", lambda e: e.tensor_reduce(out=pst[:, 0:1], in_=osq[:], axis=AX.X, op=ALU.add),
                         reads=["osq"], writes=["pst_ss"])
                    rstd_chain(pst[:, 0:1], 128.0, pst[:, 2:3], pst[:, 1:2], 1, ["pst_ss"], ["pst_r"], "pst_tmp")
                    P.op("dve", lambda e: e.scalar_tensor_tensor(out=odb[:], in0=od[:], scalar=pst[:, 2:3],
                                                                 in1=Gd[:, h * 128:(h + 1) * 128], op0=ALU.mult, op1=ALU.mult),
                         reads=["od", "pst_r", "Gd"], writes=["odb"])
                    P.op("pe", lambda e: e.transpose(pT[:, 0:128], odb[:], ident[:]), reads=["odb", "ident"], writes=["ps_T"])
                    P.op("act", lambda e: e.copy(out=mixT[:, 4 + h, t * 128:(t + 1) * 128], in_=pT[:, 0:128]),
                         reads=["ps_T"], writes=[T("mixT", t)])

                xTa = xT_all.rearrange("(kc p) n -> p kc n", p=128)
                xTo = xT_own.rearrange("(kc p) n -> p kc n", p=128)
                for gi in range(16):
                    xob = xso[gi % 2]
                    P.dma("sp", lambda e, xob=xob, gi=gi: e.dma_start(out=xob[:], in_=xTo[:, :, gi * 128:(gi + 1) * 128]),
                          writes=[T("xso", gi % 2)])
                    for tt in range(4):
                        n = gi * 4 + tt
                        sl = n % 2
                        hf = tt // 2
                        xsb = xs[hf]
                        if tt % 2 == 0:
                            P.dma("sp", lambda e, xsb=xsb, gi=gi, hf=hf: e.dma_start(
                                out=xsb[:], in_=xTa[:, :, gi * 512 + hf * 256:gi * 512 + (hf + 1) * 256]),
                                writes=[T("xs", hf)])
                        c0_ = (tt % 2) * 128
                        front(xb[sl], sq[sl], rs[sl], xsb[:, :, c0_:c0_ + 128], T("xs", hf), sl, pP[:, 0:1], "ps_P")
                        for kc in range(8):
                            P.op("pe", lambda e, kc=kc, sl=sl: e.matmul(pP[:, 0:512], lhsT=xb[sl][:, kc, :],
                                                                        rhs=WD[:, kc, 256:768], start=(kc == 0), stop=(kc == 7)),
                                 reads=[T("xb", sl), "WD"], writes=["ps_P"])
                        P.op("act", lambda e, n=n, sl=sl: e.activation(
                            out=Va[:, n, :, 0:128], in_=pP[:, 256:512].rearrange("p (h d) -> p h d", h=2),
                            func=AF.Copy, scale=rs[sl][:, 1:2]), reads=["ps_P", T("rs", sl)], writes=[T("Va", n)])
                        headnorm_T(pP[:, 0:256], "ps_P", rs[sl][:, 1:2], T("rs", sl), bvec[:, BV_KG:BV_KG + 64], 4, 64,
                                   hb[sl], sl, pT, KT[:, :, n * 128:(n + 1) * 128], T("KT", n))
                    t = gi
                    front(xb[0], sq[0], rs[0], xob[:], T("xso", t % 2), 0, pP[:, 0:1], "ps_P")
                    for kc in range(8):
                        P.op("pe", lambda e, kc=kc: e.matmul(pP[:, 0:256], lhsT=xb[0][:, kc, :], rhs=WD[:, kc, 0:256],
                                                             start=(kc == 0), stop=(kc == 7)),
                             reads=[T("xb", 0), "WD"], writes=["ps_P"])
                    headnorm_T(pP[:, 0:256], "ps_P", rs[0][:, 1:2], T("rs", 0), bvec[:, BV_QG:BV_QG + 64], 4, 64,
                               hb[0], 0, pT, QT[:, :, t * 128:(t + 1) * 128], T("QT", t))
                    if t < NQT:
                        for hh in range(2):
                            attention(t, hh)
                P.barrier()
        for hp_ in range(2):
            phase_D(hp_)

        def phase_A():
            with ExitStack() as st:
                PH = "A"

                def sb(name, shape, dt):
                    return st.enter_context(nc.sbuf_tensor(PH + "_" + name, list(shape), dt))

                def ps(name, shape, dt):
                    return st.enter_context(nc.psum_tensor(PH + "_" + name, list(shape), dt))

                Wout = sb("Wout", [128, 8, 1024], BF16)
                Wq = sb("Wq", [128, 8, 1024], BF16)
                Wxo = sb("Wxo", [128, 8, 1024], BF16)
                Wkv = sb("Wkv", [128, 8, 2048], BF16)
                knT = sb("knT", [128, 8, 256], BF16)
                Vx = sb("Vx", [128, 2, 4, 258], BF16)
                ms = sb("ms", [128, 8, 256], F32)
                xb = [sb(f"xb{i}", [128, 8, 128], BF16) for i in range(2)]
                sq = [sb(f"sq{i}", [128, 8, 128], BF16) for i in range(2)]
                rs = [sb(f"rs{i}", [128, 4], F32) for i in range(2)]
                hbA = (sb("hy", [128, 1024], F32), sb("hysq", [128, 1024], F32), sb("hnst", [128, 12], F32),
                       sb("hynb", [128, 1024], BF16))
                xt = [sb(f"xt{i}", [128, 1024], F32) for i in range(2)]
                x1 = [sb(f"x1{i}", [128, 1024], F32) for i in range(2)]
                st1 = sb("st1", [128, 4], F32)
                h2 = sb("h2", [128, 1024], BF16)
                h2T = sb("h2T", [128, 8, 128], BF16)
                qnT = sb("qnT", [128, 8, 128], BF16)
                Px = sb("Px", [128, 1024], BF16)
                rcx = sb("rcx", [128, 2], F32)
                ot = sb("ot", [128, 1024], BF16)
                oT = sb("oT", [128, 8, 128], BF16)
                one1 = sb("one1", [128, 1], F32)
                pY = ps("pY", [128, 2, 512], F32)
                pSx = ps("pSx", [128, 2, 512], F32)
                pOx = [ps(f"pOx{i}", [128, 512], F32) for i in range(2)]
                pM = ps("pM", [128, 512], F32)
                pT = ps("pT", [128, 1024], BF16)

                load_weight(Wkv, xkv_w, list(range(8)), 0, 2048, gcol=16, tok="Wkv")
                load_weight(Wout, w_out, list(range(8)), 0, 1024, gcol=None, tok="Wout")
                load_weight(Wq, xq_w, list(range(8)), 0, 1024, gcol=8, tok="Wq")
                load_weight(Wxo, xo_w, list(range(8)), 0, 1024, gcol=None, tok="Wxo")
                P.op("pool", lambda e: e.memset(Vx[:, :, :, 256:258], 1.0), writes=["Vx"])
                P.op("pool", lambda e: e.memset(one1[:], 1.0), writes=["one1"])
                P.dma("sp", lambda e: e.dma_start(out=ms[:], in_=memT.rearrange("(kc p) n -> p kc n", p=128)), writes=["ms"])
                for mt in range(2):
                    front(xb[mt], sq[mt], rs[mt], ms[:, :, mt * 128:(mt + 1) * 128], "ms", mt, pM[:, 0:1], "ps_M")
                    for half in range(2):
                        for kc in range(8):
                            P.op("pe", lambda e, kc=kc, half=half, mt=mt: e.matmul(
                                pY[:, half, :], lhsT=xb[mt][:, kc, :], rhs=Wkv[:, kc, half * 512:(half + 1) * 512],
                                start=(kc == 0), stop=(kc == 7)), reads=[T("xb", mt), "Wkv"], writes=["ps_Y"])
                    for half in range(2):
                        for kc in range(8):
                            P.op("pe", lambda e, kc=kc, half=half, mt=mt: e.matmul(
                                pSx[:, half, :], lhsT=xb[mt][:, kc, :], rhs=Wkv[:, kc, 1024 + half * 512:1024 + (half + 1) * 512],
                                start=(kc == 0), stop=(kc == 7)), reads=[T("xb", mt), "Wkv"], writes=["ps_Sx"])
                    for half in range(2):
                        P.op("act", lambda e, half=half, mt=mt: e.activation(
                            out=Vx[:, mt, 2 * half:2 * half + 2, 0:256], in_=pSx[:, half, :].rearrange("p (h d) -> p h d", h=2),
                            func=AF.Copy, scale=rs[mt][:, 1:2]), reads=["ps_Sx", T("rs", mt)], writes=["Vx"])
                    headnorm_T(pY[:, :, :].rearrange("p a b -> p (a b)"), "ps_Y", rs[mt][:, 1:2], T("rs", mt),
                               bvec[:, BV_XKG:BV_XKG + 256], 4, 256, hbA, "A", pT, knT[:, :, mt * 128:(mt + 1) * 128], "knT")

                for t in range(NT_OWN):
                    xtt, x1t = xt[t % 2], x1[t % 2]
                    P.dma("sp", lambda e, xtt=xtt, t=t: e.dma_start(out=xtt[:], in_=x_own[t * 128:(t + 1) * 128, :]),
                          writes=[T("xt", t % 2)])
                    for half in range(2):
                        for c in range(8):
                            P.op("pe", lambda e, c=c, half=half, t=t: e.matmul(
                                pY[:, half, :], lhsT=mixT[:, c, t * 128:(t + 1) * 128], rhs=Wout[:, c, half * 512:(half + 1) * 512],
                                start=(c == 0), stop=(c == 7)), reads=[T("mixT", t), "Wout"], writes=["ps_Y"])
                    P.op("dve", lambda e, xtt=xtt, x1t=x1t: e.tensor_tensor(
                        out=x1t[:], in0=xtt[:], in1=pY[:, :, :].rearrange("p a b -> p (a b)"), op=ALU.add),
                         reads=[T("xt", t % 2), "ps_Y"], writes=[T("x1", t % 2)])
                    P.op("pool", lambda e: e.memset(st1[:, 0:1], 0.0), writes=["st1"])
                    P.op("act", lambda e, x1t=x1t: e.activation(out=hbA[1][:], in_=x1t[:], func=AF.Square, accum_out=st1[:, 0:1]),
                         reads=[T("x1", t % 2), "st1"], writes=["st1", T("hn_ysq", "A")])
                    rstd_chain(st1[:, 0:1], 1024.0, st1[:, 2:3], st1[:, 1:2], 1, ["st1"], ["st1_r"], "st1_tmp")
                    P.op("act", lambda e, x1t=x1t: e.activation(out=h2[:], in_=x1t[:], func=AF.Copy, scale=st1[:, 2:3]),
                         reads=[T("x1", t % 2), "st1_r"], writes=["h2"])
                    for c in range(8):
                        P.op("pe", lambda e, c=c: e.transpose(pT[:, c * 128:(c + 1) * 128], h2[:, c * 128:(c + 1) * 128], ident[:]),
                             reads=["h2", "ident"], writes=["ps_T"])
                    P.op("dve", lambda e: e.tensor_copy(out=h2T[:].rearrange("p a b -> p (a b)"), in_=pT[:]),
                         reads=["ps_T"], writes=["h2T"])
                    for half in range(2):
                        for c in range(8):
                            P.op("pe", lambda e, c=c, half=half: e.matmul(
                                pY[:, half, :], lhsT=h2T[:, c, :], rhs=Wq[:, c, half * 512:(half + 1) * 512],
                                start=(c == 0), stop=(c == 7)), reads=["h2T", "Wq"], writes=["ps_Y"])
                    headnorm_T(pY[:, :, :].rearrange("p a b -> p (a b)"), "ps_Y", one1[:, 0:1], "one1",
                               bvec[:, BV_XQG:BV_XQG + 256], 4, 256, hbA, "A", pT, qnT[:], "qnT")
                    for h in range(4):
                        for mt in range(2):
                            for dc in range(2):
                                P.op("pe", lambda e, h=h, mt=mt, dc=dc: e.matmul(
                                    pSx[:, h // 2, (h % 2) * 256 + mt * 128:(h % 2) * 256 + (mt + 1) * 128],
                                    lhsT=knT[:, h * 2 + dc, mt * 128:(mt + 1) * 128], rhs=qnT[:, h * 2 + dc, :],
                                    start=(dc == 0), stop=(dc == 1)), reads=["knT", "qnT"], writes=["ps_Sx"])
                    P.op("act", lambda e: e.activation(out=Px[:], in_=pSx[:, :, :].rearrange("p a b -> p (a b)"),
                                                       func=AF.Exp, scale=1.0 / 16.0), reads=["ps_Sx"], writes=["Px"])
                    for h in range(4):
                        ob = h % 2
                        for mt in range(2):
                            P.op("pe", lambda e, h=h, mt=mt, ob=ob: e.matmul(
                                pOx[ob][:, 0:257], lhsT=Px[:, h * 256 + mt * 128:h * 256 + (mt + 1) * 128],
                                rhs=Vx[:, mt, h, 0:257], start=(mt == 0), stop=(mt == 1)),
                                 reads=["Px", "Vx"], writes=[T("ps_Ox", ob)])
                        P.op("dve", lambda e, ob=ob: e.reciprocal(out=rcx[:, ob:ob + 1], in_=pOx[ob][:, 256:257]),
                             reads=[T("ps_Ox", ob)], writes=[T("rcx", ob)])
                        P.op("act", lambda e, h=h, ob=ob: e.activation(out=ot[:, h * 256:(h + 1) * 256], in_=pOx[ob][:, 0:256],
                                                                       func=AF.Copy, scale=rcx[:, ob:ob + 1]),
                             reads=[T("ps_Ox", ob), T("rcx", ob)], writes=["ot"])
                    for c in range(8):
                        P.op("pe", lambda e, c=c: e.transpose(pT[:, c * 128:(c + 1) * 128], ot[:, c * 128:(c + 1) * 128], ident[:]),
                             reads=["ot", "ident"], writes=["ps_T"])
                    P.op("dve", lambda e: e.tensor_copy(out=oT[:].rearrange("p a b -> p (a b)"), in_=pT[:]),
                         reads=["ps_T"], writes=["oT"])
                    for half in range(2):
                        for c in range(8):
                            P.op("pe", lambda e, c=c, half=half: e.matmul(
                                pY[:, half, :], lhsT=oT[:, c, :], rhs=Wxo[:, c, half * 512:(half + 1) * 512],
                                start=(c == 0), stop=(c == 7)), reads=["oT", "Wxo"], writes=["ps_Y"])
                    P.op("dve", lambda e, xtt=xtt, x1t=x1t: e.tensor_tensor(
                        out=xtt[:], in0=x1t[:], in1=pY[:, :, :].rearrange("p a b -> p (a b)"), op=ALU.add),
                         reads=[T("x1", t % 2), "ps_Y"], writes=[T("xt", t % 2)])
                    P.dma("sp", lambda e, xtt=xtt, t=t: e.dma_start(out=x2s_d[t * 128:(t + 1) * 128, :], in_=xtt[:]),
                          reads=[T("xt", t % 2)], writes=[T("x2s", t)])
                P.barrier()
        phase_A()
        if debug:
            P.dma("sp", lambda e: e.dma_start(out=dbg_mix, in_=mixT[:].rearrange("p a b -> p (a b)")),
                  reads=[T("mixT", t) for t in range(16)], writes=["dbg_mix"])
            P.barrier()
        mst.close()

        def phase_B():
            with ExitStack() as st:
                PH = "B"

                def sb(name, shape, dt):
                    return st.enter_context(nc.sbuf_tensor(PH + "_" + name, list(shape), dt))

                def ps(name, shape, dt):
                    return st.enter_context(nc.psum_tensor(PH + "_" + name, list(shape), dt))

                Wg = sb("Wg", [128, 8, FFN], BF16)
                Wu = sb("Wu", [128, 8, FFN], BF16)
                Wd = sb("Wd", [128, NFC, 1024], BF16)
                xg = sb("xg", [128, 4, 1024], F32)
                junk = sb("junk", [128, 1024], F32)
                h3 = sb("h3", [128, 1024], BF16)
                h3T = sb("h3T", [128, 8, 512], BF16)
                actT = sb("actT", [128, NFC, 512], BF16)
                sgt = [sb(f"sgt{i}", [128, 512], F32) for i in range(2)]
                st3 = sb("st3", [128, 4], F32)
                pG = [ps(f"pG{i}", [128, 512], F32) for i in range(2)]
                pU = [ps(f"pU{i}", [128, 512], F32) for i in range(2)]
                pD = ps("pD", [128, 2, 512], F32)
                pT = ps("pT", [128, 1024], BF16)

                load_weight(Wg, w_gate, list(range(8)), 0, FFN, gcol=24, tok="Wg")
                load_weight(Wu, w_up, list(range(8)), 0, FFN, gcol=24, tok="Wu")
                load_weight(Wd, w_down, list(range(NFC)), 0, 1024, gcol=None, tok="Wd")
                for g in range(4):
                    P.dma("sp", lambda e, g=g: e.dma_start(out=xg[:], in_=x2s_d[g * 512:(g + 1) * 512, :].rearrange(
                        "(t p) f -> p t f", p=128)), reads=[T("x2s", 4 * g + i) for i in range(4)], writes=["xg"])
                    for tt in range(4):
                        P.op("pool", lambda e: e.memset(st3[:, 0:1], 0.0), writes=["st3"])
                        P.op("act", lambda e, tt=tt: e.activation(out=junk[:], in_=xg[:, tt, :], func=AF.Square,
                                                                  accum_out=st3[:, 0:1]), reads=["xg", "st3"], writes=["st3", "junk"])
                        rstd_chain(st3[:, 0:1], 1024.0, st3[:, 2:3], st3[:, 1:2], 1, ["st3"], ["st3_r"], "st3_tmp")
                        P.op("act", lambda e, tt=tt: e.activation(out=h3[:], in_=xg[:, tt, :], func=AF.Copy, scale=st3[:, 2:3]),
                             reads=["xg", "st3_r"], writes=["h3"])
                        for c in range(8):
                            P.op("pe", lambda e, c=c: e.transpose(pT[:, c * 128:(c + 1) * 128], h3[:, c * 128:(c + 1) * 128], ident[:]),
                                 reads=["h3", "ident"], writes=["ps_T"])
                        P.op("dve", lambda e, tt=tt: e.tensor_copy(out=h3T[:, :, tt * 128:(tt + 1) * 128],
                                                                   in_=pT[:].rearrange("p (c n) -> p c n", c=8)),
                             reads=["ps_T"], writes=["h3T"])
                    for fc in range(NFC):
                        b_ = fc % 2
                        for c in range(8):
                            P.op("pe", lambda e, c=c, fc=fc, b_=b_: e.matmul(pG[b_][:], lhsT=Wg[:, c, fc * 128:(fc + 1) * 128],
                                                                             rhs=h3T[:, c, :], start=(c == 0), stop=(c == 7)),
                                 reads=["h3T", "Wg"], writes=[T("ps_G", b_)])
                        for c in range(8):
                            P.op("pe", lambda e, c=c, fc=fc, b_=b_: e.matmul(pU[b_][:], lhsT=Wu[:, c, fc * 128:(fc + 1) * 128],
                                                                             rhs=h3T[:, c, :], start=(c == 0), stop=(c == 7)),
                                 reads=["h3T", "Wu"], writes=[T("ps_U", b_)])
                        P.op("act", lambda e, b_=b_: e.activation(out=sgt[b_][:], in_=pG[b_][:], func=AF.Silu),
                             reads=[T("ps_G", b_)], writes=[T("sgt", b_)])
                        P.op("dve", lambda e, b_=b_, fc=fc: e.tensor_tensor(out=actT[:, fc, :], in0=sgt[b_][:], in1=pU[b_][:],
                                                                            op=ALU.mult),
                             reads=[T("sgt", b_), T("ps_U", b_)], writes=[T("actT", fc)])
                    for tt in range(4):
                        for half in range(2):
                            for fc in range(NFC):
                                P.op("pe", lambda e, fc=fc, half=half, tt=tt: e.matmul(
                                    pD[:, half, :], lhsT=actT[:, fc, tt * 128:(tt + 1) * 128],
                                    rhs=Wd[:, fc, half * 512:(half + 1) * 512], start=(fc == 0), stop=(fc == NFC - 1)),
                                     reads=[T("actT", fc), "Wd"], writes=["ps_D"])
                        P.op("dve", lambda e, tt=tt: e.tensor_tensor(out=xg[:, tt, :], in0=xg[:, tt, :],
                                                                     in1=pD[:, :, :].rearrange("p a b -> p (a b)"), op=ALU.add),
                             reads=["xg", "ps_D"], writes=["xg"])
                    P.dma("sp", lambda e, g=g: e.dma_start(out=out_d[g * 512:(g + 1) * 512, :].rearrange(
                        "(t p) f -> p t f", p=128), in_=xg[:]), reads=["xg"], writes=[T("out", g)])
                P.op("sp", lambda e: e.nop(), reads=[T("out", g) for g in range(4)] + (["dbg_mix"] if debug else []))
        phase_B()
        _mx = int(_os.environ.get("KMAXOPS", "0"))
        if _mx:
            print("total ops", len(P.ops))
            P.ops = P.ops[:_mx]
        P.emit()
    return nc


def _prep_inputs(x, mem, norm_mix, w_in, diff_q_gain, diff_k_gain, diff_lambda, diff_subln, group_scale, w_out,
                 norm_x, norm_mem, xq, xkv, xq_gain, xk_gain, xo, norm_ffn, w_gate, w_up, w_down):
    f = lambda a: np.ascontiguousarray(np.asarray(a, dtype=np.float32))
    x = f(x)
    mem = f(mem)
    gammas = np.zeros((128, 32), np.float32)
    for i, g in enumerate([norm_mix, norm_x, norm_mem, norm_ffn]):
        gammas[:, i * 8:(i + 1) * 8] = f(g).reshape(8, 128).T
    bv = np.zeros((BV_N,), np.float32)
    bv[BV_QG:BV_QG + 64] = f(diff_q_gain).reshape(-1)
    bv[BV_KG:BV_KG + 64] = f(diff_k_gain).reshape(-1)
    bv[BV_LAM:BV_LAM + 256] = f(diff_lambda).reshape(-1)
    bv[BV_SUBLN:BV_SUBLN + 128] = f(diff_subln).reshape(-1)
    bv[BV_GSC:BV_GSC + 1024] = f(group_scale).reshape(-1)
    bv[BV_XQG:BV_XQG + 256] = f(xq_gain).reshape(-1)
    bv[BV_XKG:BV_XKG + 256] = f(xk_gain).reshape(-1)
    bvec = np.ascontiguousarray(np.broadcast_to(bv[None, :], (128, BV_N)))
    shared = {
        "w_in": f(w_in)[0], "w_out": f(w_out)[0], "xq": f(xq)[0], "xkv": f(xkv)[0], "xo": f(xo)[0],
        "w_gate": f(w_gate)[0], "w_up": f(w_up)[0], "w_down": f(w_down)[0],
        "gammas": gammas, "bvec": bvec, "ident": np.eye(128, dtype=np.float32),
    }
    in_maps = []
    for c in range(8):
        b, j = c // 4, c % 4
        xb_ = x[b]
        own = xb_.reshape(16, 4, 128, 1024)[:, j].reshape(2048, 1024)
        ct, am = _core_tables(j)
        m = dict(shared)
        m["xT_all"] = np.ascontiguousarray(xb_.T)
        m["xT_own"] = np.ascontiguousarray(own.T)
        m["x_own"] = np.ascontiguousarray(own)
        m["memT"] = np.ascontiguousarray(mem[b].T)
        m["ctabR"] = ct
        m["amask"] = am
        in_maps.append(m)
    return in_maps


_NC_CACHE = {}


def kernel(**inputs):
    in_maps = _prep_inputs(**inputs)
    if "nc" not in _NC_CACHE:
        _NC_CACHE["nc"] = build_nc()
    nc = _NC_CACHE["nc"]
    res = run_bass_kernel_spmd(nc, in_maps, core_ids=list(range(8)))
    out = np.zeros((2, SEQ, D_MODEL), np.float32)
    for c in range(8):
        b, j = c // 4, c % 4
        o = np.asarray(res.results[c]["out"]).reshape(16, 128, 1024)
        out[b].reshape(16, 4, 128, 1024)[:, j] = o
    return out
```
